# Optimizing a Trainium2 kernel written in Bass

```python
import math
import jax, jax.numpy as jnp
from jax import lax
import numpy as np

D_MODEL = 2048
BATCH = 2
SEQ = 8192
DEPTH = 1

MEM_LEN = 256
D_FF = 256 * math.ceil(8 * D_MODEL / 3 / 256)
BRANCH_WIDTH = D_MODEL // 2
N_BRANCH = 3
CONV_WIDTH = BRANCH_WIDTH
CONV_K = 3
NSA_HEAD_DIM = 64
NSA_HEADS = BRANCH_WIDTH // NSA_HEAD_DIM
NSA_GROUPS = 4
NSA_HPG = NSA_HEADS // NSA_GROUPS
NSA_KV = NSA_GROUPS * NSA_HEAD_DIM
CMP_LEN = 32
CMP_STRIDE = 16
CMP_HIDDEN = 4 * NSA_HEAD_DIM
SLC_LEN = 64
SLC_TOP = 16
WIN = 512
Q_BLOCK = 128
MEM_HEADS = 4
MEM_HEAD_DIM = BRANCH_WIDTH // MEM_HEADS
EPS = 1e-6
NEG_INF = -1e30
FORCE_SCORE = 1e9
IN_SPLITS = (CONV_WIDTH, CONV_WIDTH, CONV_WIDTH,
             NSA_HEADS * NSA_HEAD_DIM,
             6 * NSA_KV,
             3 * NSA_HEADS,
             MEM_HEADS * MEM_HEAD_DIM,
             N_BRANCH * D_MODEL)
D_IN = sum(IN_SPLITS)

kernel_name = 'hybrid_conv_nsa_mem_macaron'


def rms_norm(x, g):
    xf = x.astype(jnp.float32)
    y = xf * lax.rsqrt(jnp.mean(xf * xf, axis=-1, keepdims=True) + EPS)
    return (y * g.astype(jnp.float32)).astype(x.dtype)


def swiglu(h, w_gate, w_up, w_down):
    return (jax.nn.silu(h @ w_gate) * (h @ w_up)) @ w_down


def masked_softmax(s, valid):
    s = jnp.where(valid, s.astype(jnp.float32), NEG_INF)
    p = jax.nn.softmax(s, axis=-1)
    return jnp.where(valid, p, 0.0)


def short_conv_mixer(b, c, u, conv_w):
    z = c * u
    y = lax.conv_general_dilated(z, conv_w[:, None, :], window_strides=(1,),
                                 padding=[(CONV_K - 1, 0)],
                                 dimension_numbers=('NWC', 'WIO', 'NWC'),
                                 feature_group_count=CONV_WIDTH)
    return b * y


def compress_blocks(t, pe, w1, w2):
    B, S = t.shape[:2]
    n_c = (S - CMP_LEN) // CMP_STRIDE + 1
    idx = jnp.arange(n_c)[:, None] * CMP_STRIDE + jnp.arange(CMP_LEN)[None, :]
    blk = t[:, idx] + pe[None, None, :, None, :]
    blk = jnp.moveaxis(blk, 3, 2).reshape(B, n_c, NSA_GROUPS, CMP_LEN * NSA_HEAD_DIM)
    return jax.nn.silu(blk @ w1) @ w2


def selection_map(n_c, n_s):
    ratio = SLC_LEN // CMP_STRIDE
    offs = np.arange(-(CMP_LEN // CMP_STRIDE - 1), ratio)
    ci = ratio * np.arange(n_s)[:, None] + offs[None, :]
    cs = ci * CMP_STRIDE
    ss = np.arange(n_s)[:, None] * SLC_LEN
    ov = np.clip(np.minimum(cs + CMP_LEN, ss + SLC_LEN) - np.maximum(cs, ss), 0, None)
    ok = (ci >= 0) & (ci < n_c)
    w = np.where(ok, ov / CMP_LEN, 0.0).astype(np.float32)
    return jnp.asarray(np.clip(ci, 0, n_c - 1)), jnp.asarray(w)


def nsa_attention(q, kv, gate_logits, g_q, g_kc, g_ks, g_kw,
                  pe_k, w1_k, w2_k, pe_v, w1_v, w2_v):
    B, S = q.shape[:2]
    G, HPG, DH = NSA_GROUPS, NSA_HPG, NSA_HEAD_DIM
    scale = DH ** -0.5
    q = rms_norm(q.reshape(B, S, NSA_HEADS, DH), g_q).reshape(B, S, G, HPG, DH)
    kc, vc, ks, vs, kw, vw = [t.reshape(B, S, G, DH) for t in jnp.split(kv, 6, axis=-1)]
    k_c = rms_norm(compress_blocks(kc, pe_k, w1_k, w2_k), g_kc)
    v_c = compress_blocks(vc, pe_v, w1_v, w2_v)
    n_c = k_c.shape[1]
    n_s = S // SLC_LEN
    n_top = min(SLC_TOP, n_s)
    k_s = rms_norm(ks, g_ks).reshape(B, n_s, SLC_LEN, G, DH).transpose(0, 3, 1, 2, 4)
    v_s = vs.reshape(B, n_s, SLC_LEN, G, DH).transpose(0, 3, 1, 2, 4)
    pad = ((0, 0), (WIN, 0), (0, 0), (0, 0))
    k_w = jnp.pad(rms_norm(kw, g_kw), pad)
    v_w = jnp.pad(vw, pad)
    gates = jax.nn.sigmoid(gate_logits.reshape(B, S, G, HPG, 3))
    slc_idx, slc_w = selection_map(n_c, n_s)
    cmp_end = jnp.arange(n_c) * CMP_STRIDE + CMP_LEN - 1
    blk_ids = jnp.arange(n_s)
    b_ix = jnp.arange(B)[:, None, None, None]
    g_ix = jnp.arange(G)[None, None, :, None]

    def block(c):
        t0 = c * Q_BLOCK
        t = t0 + jnp.arange(Q_BLOCK)
        qb = lax.dynamic_slice_in_dim(q, t0, Q_BLOCK, axis=1)
        s_c = jnp.einsum('bqghd,bngd->bqghn', qb, k_c) * scale
        valid_c = cmp_end[None, :] <= t[:, None]
        p_c = masked_softmax(s_c, valid_c[None, :, None, None, :])
        o_c = jnp.einsum('bqghn,bngd->bqghd', p_c.astype(v_c.dtype), v_c)
        imp_c = jnp.sum(p_c, axis=3)
        imp_s = jnp.sum(imp_c[..., slc_idx] * slc_w, axis=-1)
        cur = t // SLC_LEN
        valid_s = blk_ids[None, :] <= cur[:, None]
        forced = ((blk_ids[None, :] == 0) | (blk_ids[None, :] == cur[:, None])
                  | (blk_ids[None, :] == cur[:, None] - 1))
        score = jnp.where(forced[None, :, None, :], FORCE_SCORE,
                          jnp.where(valid_s[None, :, None, :], imp_s, NEG_INF))
        _, sel = lax.top_k(score, n_top)
        k_sel = k_s[b_ix, g_ix, sel].reshape(B, Q_BLOCK, G, n_top * SLC_LEN, DH)
        v_sel = v_s[b_ix, g_ix, sel].reshape(B, Q_BLOCK, G, n_top * SLC_LEN, DH)
        kpos = (sel[..., None] * SLC_LEN + jnp.arange(SLC_LEN)).reshape(B, Q_BLOCK, G, n_top * SLC_LEN)
        valid_sel = kpos <= t[None, :, None, None]
        s_s = jnp.einsum('bqghd,bqgkd->bqghk', qb, k_sel) * scale
        p_s = masked_softmax(s_s, valid_sel[:, :, :, None, :])
        o_s = jnp.einsum('bqghk,bqgkd->bqghd', p_s.astype(v_sel.dtype), v_sel)
        kw_b = lax.dynamic_slice_in_dim(k_w, t0, Q_BLOCK + WIN, axis=1)
        vw_b = lax.dynamic_slice_in_dim(v_w, t0, Q_BLOCK + WIN, axis=1)
        pos = t0 - WIN + jnp.arange(Q_BLOCK + WIN)
        valid_w = (pos[None, :] >= 0) & (pos[None, :] <= t[:, None]) & (pos[None, :] > t[:, None] - WIN)
        s_w = jnp.einsum('bqghd,bkgd->bqghk', qb, kw_b) * scale
        p_w = masked_softmax(s_w, valid_w[None, :, None, None, :])
        o_w = jnp.einsum('bqghk,bkgd->bqghd', p_w.astype(vw_b.dtype), vw_b)
        gb = lax.dynamic_slice_in_dim(gates, t0, Q_BLOCK, axis=1)
        o = gb[..., 0:1] * o_c + gb[..., 1:2] * o_s + gb[..., 2:3] * o_w
        return o.reshape(B, Q_BLOCK, NSA_HEADS * DH)

    out = lax.map(block, jnp.arange(S // Q_BLOCK))
    return jnp.moveaxis(out, 0, 1).reshape(B, S, NSA_HEADS * DH)


def memory_cross_attention(q, mem_h, w_kv, g_q, g_k):
    B, S = q.shape[:2]
    M = mem_h.shape[1]
    km, vm = jnp.split(mem_h @ w_kv, 2, axis=-1)
    km = rms_norm(km.reshape(B, M, MEM_HEADS, MEM_HEAD_DIM), g_k)
    vm = vm.reshape(B, M, MEM_HEADS, MEM_HEAD_DIM)
    q = rms_norm(q.reshape(B, S, MEM_HEADS, MEM_HEAD_DIM), g_q)
    s = jnp.einsum('bshd,bmhd->bhsm', q, km).astype(jnp.float32) * MEM_HEAD_DIM ** -0.5
    p = jax.nn.softmax(s, axis=-1).astype(vm.dtype)
    return jnp.einsum('bhsm,bmhd->bshd', p, vm).reshape(B, S, BRANCH_WIDTH)


def setup_inputs(seed: int = 0) -> dict:
    key = jax.random.key(seed)
    ks = iter(jax.random.split(key, 40))

    def w(shape, fan_in):
        return jax.random.normal(next(ks), (DEPTH,) + shape, jnp.float32) * fan_in ** -0.5

    def gain(n):
        return 1.0 + 0.02 * jax.random.normal(next(ks), (DEPTH, n), jnp.float32)

    x = jax.random.normal(next(ks), (BATCH, SEQ, D_MODEL), jnp.float32)
    mem = jax.random.normal(next(ks), (BATCH, MEM_LEN, D_MODEL), jnp.float32)
    return {
        'x': x,
        'mem': mem,
        'ffn1_norm': gain(D_MODEL),
        'ffn1_w_gate': w((D_MODEL, D_FF), D_MODEL),
        'ffn1_w_up': w((D_MODEL, D_FF), D_MODEL),
        'ffn1_w_down': w((D_FF, D_MODEL), D_FF),
        'mix_norm': gain(D_MODEL),
        'mem_norm': gain(D_MODEL),
        'w_in': w((D_MODEL, D_IN), D_MODEL),
        'conv_w': w((CONV_K, CONV_WIDTH), CONV_K),
        'nsa_q_norm': gain(NSA_HEAD_DIM),
        'nsa_kc_norm': gain(NSA_HEAD_DIM),
        'nsa_ks_norm': gain(NSA_HEAD_DIM),
        'nsa_kw_norm': gain(NSA_HEAD_DIM),
        'cmp_pe_k': 0.1 * jax.random.normal(next(ks), (DEPTH, CMP_LEN, NSA_HEAD_DIM), jnp.float32),
        'cmp_w1_k': w((CMP_LEN * NSA_HEAD_DIM, CMP_HIDDEN), CMP_LEN * NSA_HEAD_DIM),
        'cmp_w2_k': w((CMP_HIDDEN, NSA_HEAD_DIM), CMP_HIDDEN),
        'cmp_pe_v': 0.1 * jax.random.normal(next(ks), (DEPTH, CMP_LEN, NSA_HEAD_DIM), jnp.float32),
        'cmp_w1_v': w((CMP_LEN * NSA_HEAD_DIM, CMP_HIDDEN), CMP_LEN * NSA_HEAD_DIM),
        'cmp_w2_v': w((CMP_HIDDEN, NSA_HEAD_DIM), CMP_HIDDEN),
        'w_mem_kv': w((D_MODEL, 2 * BRANCH_WIDTH), D_MODEL),
        'mem_q_norm': gain(MEM_HEAD_DIM),
        'mem_k_norm': gain(MEM_HEAD_DIM),
        'w_branch': w((N_BRANCH, BRANCH_WIDTH, D_MODEL), BRANCH_WIDTH),
        'w_o': w((D_MODEL, D_MODEL), D_MODEL),
        'ffn2_norm': gain(D_MODEL),
        'ffn2_w_gate': w((D_MODEL, D_FF), D_MODEL),
        'ffn2_w_up': w((D_MODEL, D_FF), D_MODEL),
        'ffn2_w_down': w((D_FF, D_MODEL), D_FF),
    }


def reference(x, mem, ffn1_norm, ffn1_w_gate, ffn1_w_up, ffn1_w_down, mix_norm, mem_norm,
              w_in, conv_w, nsa_q_norm, nsa_kc_norm, nsa_ks_norm, nsa_kw_norm,
              cmp_pe_k, cmp_w1_k, cmp_w2_k, cmp_pe_v, cmp_w1_v, cmp_w2_v,
              w_mem_kv, mem_q_norm, mem_k_norm, w_branch, w_o,
              ffn2_norm, ffn2_w_gate, ffn2_w_up, ffn2_w_down):
    B, S, D = x.shape
    split_at = np.cumsum(IN_SPLITS)[:-1].tolist()
    for l in range(DEPTH):
        h = rms_norm(x, ffn1_norm[l])
        x = x + 0.5 * swiglu(h, ffn1_w_gate[l], ffn1_w_up[l], ffn1_w_down[l])
        h = rms_norm(x, mix_norm[l])
        z = h @ w_in[l]
        b_g, c_g, u, q_nsa, kv_nsa, g_nsa, q_mem, g_merge = jnp.split(z, split_at, axis=-1)
        y_conv = short_conv_mixer(b_g, c_g, u, conv_w[l])
        y_nsa = nsa_attention(q_nsa, kv_nsa, g_nsa, nsa_q_norm[l], nsa_kc_norm[l],
                              nsa_ks_norm[l], nsa_kw_norm[l], cmp_pe_k[l], cmp_w1_k[l],
                              cmp_w2_k[l], cmp_pe_v[l], cmp_w1_v[l], cmp_w2_v[l])
        y_mem = memory_cross_attention(q_mem, rms_norm(mem, mem_norm[l]), w_mem_kv[l],
                                       mem_q_norm[l], mem_k_norm[l])
        ys = jnp.stack([y_conv, y_nsa, y_mem], axis=2)
        yb = jnp.einsum('bsnc,ncd->bsnd', ys, w_branch[l])
        gate = jax.nn.sigmoid(g_merge.reshape(B, S, N_BRANCH, D))
        merged = jnp.einsum('bsnd,bsnd->bsd', gate, yb)
        x = x + merged @ w_o[l]
        h = rms_norm(x, ffn2_norm[l])
        x = x + 0.5 * swiglu(h, ffn2_w_gate[l], ffn2_w_up[l], ffn2_w_down[l])
    return x
```

```python
import math
import numpy as np
from contextlib import ExitStack
import concourse.bass as bass
import concourse.mybir as mybir
from concourse.bass_utils import run_bass_kernel_spmd

F32 = mybir.dt.float32
BF16 = mybir.dt.bfloat16
AF = mybir.ActivationFunctionType
ALU = mybir.AluOpType
AX = mybir.AxisListType

PE, ACT, DVE, POOL, SP = "tensor", "scalar", "vector", "gpsimd", "sync"
ENGS = (PE, ACT, DVE, POOL, SP)
EPOCH = 30000
SAME_ENGINE_SYNC = True

D = 2048
DFF = 5632
NFC = 44
SEQ = 8192
NT = 16
NST = 4
EPS = 1e-6
XW = 1560
X_VS, X_VW, X_KS, X_KW, X_KC, X_VC, X_ZC = 0, 260, 520, 776, 1032, 1288, 1544
NEGB = -30000.0


class Buf:
    __slots__ = ("name", "w", "r")

    def __init__(self, name=""):
        self.name = name
        self.w = None
        self.r = []


class Prog:
    def __init__(self, nc, es, n_dma_sems=48):
        self.nc = nc
        self.es = es
        self.ops = {e: [] for e in ENGS}
        self.cnt = {e: 0 for e in ENGS}
        self.esems = {e: [] for e in ENGS}
        self.seen = {e: {} for e in ENGS}
        self.dsems = [es.enter_context(nc.semaphore(f"dq{i}")) for i in range(n_dma_sems)]
        self.dcnt = [0] * n_dma_sems
        self.drr = 0

    def _esem(self, e, epoch):
        while len(self.esems[e]) <= epoch:
            self.esems[e].append(self.es.enter_context(
                self.nc.semaphore(f"p_{e}_{len(self.esems[e])}")))
        return self.esems[e][epoch]

    def _deps(self, e, reads, writes):
        toks = []
        for b in reads:
            if b.w is not None:
                toks.append(b.w)
        for b in writes:
            if b.w is not None:
                toks.append(b.w)
            toks.extend(b.r)
        need = {}
        for (sem, val, eng) in toks:
            if eng == e and (e == PE or not SAME_ENGINE_SYNC):
                continue
            if self.seen[e].get(id(sem), (None, 0))[1] >= val:
                continue
            if id(sem) not in need or need[id(sem)][1] < val:
                need[id(sem)] = (sem, val)
        for k, (sem, val) in need.items():
            self.seen[e][k] = (sem, val)
        return list(need.values())

    def _commit(self, tok, reads, writes):
        for b in reads:
            b.r.append(tok)
            if len(b.r) > 64:
                b.r = b.r[-64:] if False else b.r
        for b in writes:
            b.w = tok
            b.r = []

    def op(self, e, fn, reads=(), writes=()):
        waits = self._deps(e, reads, writes)
        self.cnt[e] += 1
        epoch, val = divmod(self.cnt[e] - 1, EPOCH)
        val += 1
        sem = self._esem(e, epoch)

        def emit(eng, waits=waits, fn=fn, sem=sem):
            for (s, v) in waits:
                eng.wait_ge(s, v)
            fn(eng).then_inc(sem, 1)
        self.ops[e].append(emit)
        tok = (sem, val, e)
        self._commit(tok, reads, writes)
        return tok

    def dma(self, q, out, in_, reads=(), writes=(), **kw):
        waits = self._deps(q, reads, writes)
        i = self.drr
        self.drr = (self.drr + 1) % len(self.dsems)
        sem = self.dsems[i]
        prev = self.dcnt[i] * 16
        self.dcnt[i] += 1
        val = self.dcnt[i] * 16
        if prev > 0 and self.seen[q].get(id(sem), (None, 0))[1] < prev:
            waits.append((sem, prev))
            self.seen[q][id(sem)] = (sem, prev)

        def emit(eng, waits=waits, sem=sem, out=out, in_=in_, kw=kw):
            for (s, v) in waits:
                eng.wait_ge(s, v)
            eng.dma_start(out=out, in_=in_, **kw).then_inc(sem, 16)
        self.ops[q].append(emit)
        tok = (sem, val, "dma")
        self._commit(tok, reads, writes)
        return tok

    def custom(self, q, fn, sem_inc, reads=(), writes=()):
        waits = self._deps(q, reads, writes)
        i = self.drr
        self.drr = (self.drr + 1) % len(self.dsems)
        sem = self.dsems[i]
        prev = self.dcnt[i] * 16
        self.dcnt[i] += 1
        val = self.dcnt[i] * 16
        if prev > 0 and self.seen[q].get(id(sem), (None, 0))[1] < prev:
            waits.append((sem, prev))
            self.seen[q][id(sem)] = (sem, prev)

        def emit(eng, waits=waits, sem=sem, fn=fn):
            for (s, v) in waits:
                eng.wait_ge(s, v)
            fn(eng).then_inc(sem, sem_inc)
        self.ops[q].append(emit)
        tok = (sem, val, "dma")
        self._commit(tok, reads, writes)
        return tok

    def barrier(self):
        toks = []
        for e in ENGS:
            c = self.cnt[e]
            if c:
                epoch, val = divmod(c - 1, EPOCH)
                toks.append((self.esems[e][epoch], val + 1, e))
        for i, s in enumerate(self.dsems):
            if self.dcnt[i]:
                toks.append((s, self.dcnt[i] * 16, "dma"))
        for e in ENGS:
            waits = []
            for (sem, val, eng) in toks:
                if eng == e:
                    continue
                if self.seen[e].get(id(sem), (None, 0))[1] >= val:
                    continue
                waits.append((sem, val))
                self.seen[e][id(sem)] = (sem, val)

            def emit(eng, waits=waits):
                for (s, v) in waits:
                    eng.wait_ge(s, v)
            self.ops[e].append(emit)

    def wait_all(self, e, bufs):
        waits = self._deps(e, bufs, ())

        def emit(eng, waits=waits):
            for (s, v) in waits:
                eng.wait_ge(s, v)
        self.ops[e].append(emit)

    def emit(self):
        nc = self.nc
        with nc.Block() as block:
            @block.tensor
            def _(eng):
                for f in self.ops[PE]:
                    f(eng)

            @block.scalar
            def _(eng):
                for f in self.ops[ACT]:
                    f(eng)

            @block.vector
            def _(eng):
                for f in self.ops[DVE]:
                    f(eng)

            @block.gpsimd
            def _(eng):
                for f in self.ops[POOL]:
                    f(eng)

            @block.sync
            def _(eng):
                for f in self.ops[SP]:
                    f(eng)


class T:
    def __init__(self, t, n=1, name=""):
        self.t = t
        self.b = [Buf(f"{name}{i}") for i in range(n)]

    def __getitem__(self, k):
        return self.t[k]


def _lhsT_tiles(w, kchunks, ncols_chunk=128):
    K, N = w.shape
    return np.ascontiguousarray(
        w.reshape(K // 128, 128, N // ncols_chunk, ncols_chunk).transpose(2, 1, 0, 3))


PCOL = {}


def _param_cols():
    off = 0
    for name, n in [("g_ffn1", 16), ("g_mix", 16), ("g_ffn2", 16), ("g_mem", 16),
                    ("g_q", 1), ("g_ks", 1), ("g_kw", 1), ("g_kc", 1),
                    ("g_mq", 2), ("g_mk", 2), ("convw", 24), ("selc", 4), ("pek", 32), ("pev", 32)]:
        PCOL[name] = (off, n)
        off += n
    return off


NPAR = _param_cols()


def build_constants(j):
    c = {}
    f32 = np.float32
    k = np.arange(128)[:, None]
    q = np.arange(128)[None, :]
    mw = np.zeros((8, 128, 128), f32)
    for dd in range(-4, 4):
        delta = dd - j
        if delta > 0 or delta < -4:
            m = np.full((128, 128), NEGB)
        elif delta == 0:
            m = np.where(k <= q, 0.0, NEGB)
        elif delta == -4:
            m = np.where(k > q, 0.0, NEGB)
        else:
            m = np.zeros((128, 128))
        mw[dd + 4] = m
    c["maskw"] = np.ascontiguousarray(mw.transpose(1, 0, 2)).reshape(128, 8 * 128).astype(f32)
    mc = np.zeros((128, 16, 2, 128), f32)
    for i in range(16):
        qi = 4 * i + j
        for slot, ch in enumerate(cmp_mask_chunks(i)):
            n = 128 * ch + np.arange(128)[:, None]
            t = 128 * qi + q
            mc[:, i, slot, :] = np.where((16 * n + 31 <= t) & (n <= 510), 0.0, NEGB)
    c["maskc"] = mc.reshape(128, 16 * 2 * 128)
    fb = np.zeros((128, 16, 128), f32)
    blk = np.arange(128)[None, :]
    for i in range(16):
        qi = 4 * i + j
        cur = 2 * qi + (np.arange(128)[:, None] >= 64)
        v = np.zeros((128, 128), f32)
        v = np.where(blk > cur, -1e30, v)
        v = v + np.where(blk == cur, 2e9, 0.0) + np.where(blk == cur - 1, 4e9, 0.0) + np.where(blk == 0, 1e9, 0.0)
        fb[:, i, :] = v
    c["fbias"] = fb.reshape(128, 16 * 128)
    return c


def cmp_chunks(i):
    return list(range(0, (32 * i + 30) // 128 + 1))


def cmp_mask_chunks(i):
    lo = max(32 * i - 1, 0) // 128
    hi = (32 * i + 30) // 128
    return list(range(lo, hi + 1))


def shared_constants():
    f32 = np.float32
    em = np.zeros((128, 64, 128), f32)
    for kt in range(64):
        em[2 * kt, kt, 0:64] = 1.0
        em[2 * kt + 1, kt, 64:128] = 1.0
    mp = np.zeros((512, 129), f32)
    wts = (0.5, 1.0, 1.0, 1.0, 0.5)
    for b_ in range(128):
        for o in range(-1, 4):
            n = 4 * b_ + o
            if 0 <= n <= 510:
                mp[n, b_] = wts[o + 1]
    mp[0:511, 128] = 1.0
    mapa = np.ascontiguousarray(mp.reshape(4, 128, 129).transpose(1, 0, 2)).reshape(128, 4 * 129)
    return {"emat": em.reshape(128, 64 * 128), "mapa": mapa}


def prep_inputs(inp):
    f32 = np.float32
    sh = {}
    g = lambda n: np.asarray(inp[n], f32)[0]
    w_in = g("w_in")
    for tag in ("ffn1", "ffn2"):
        wg = _lhsT_tiles(g(f"{tag}_w_gate"), 16)
        wu = _lhsT_tiles(g(f"{tag}_w_up"), 16)
        sh[f"wgu_{tag}"] = np.ascontiguousarray(np.stack([wg, wu], axis=2)).reshape(NFC, 128, 2 * 16 * 128)
        sh[f"wd_{tag}"] = _lhsT_tiles(g(f"{tag}_w_down"), 44).reshape(16, 128, 44 * 128)
    qcols = []
    for gp in range(2):
        for hl in range(4):
            for par in range(2):
                h = 4 * (2 * gp + par) + hl
                qcols.extend(range(3072 + 64 * h, 3072 + 64 * h + 64))
    kv = 4096
    seg = lambda a: list(range(kv + a, kv + a + 256))
    ncols = qcols + seg(512) + seg(1024) + seg(768) + seg(1280) + seg(0) + seg(256)
    wn = w_in[:, ncols]
    sh["wn"] = np.ascontiguousarray(wn.reshape(16, 128, 5, 512).transpose(2, 1, 0, 3)).reshape(5, 128, 16 * 512)
    sh["wng"] = np.ascontiguousarray(w_in[:, 5632:5680].reshape(16, 128, 48).transpose(1, 0, 2)).reshape(128, 16 * 48)
    fcols = list(range(0, 3072)) + list(range(5680, 6704)) + list(range(6704, 12848))
    sh["wf"] = _lhsT_tiles(w_in[:, fcols], 16).reshape(80, 128, 16 * 128)
    wbr = g("w_branch")
    sh["wbr"] = np.ascontiguousarray(
        wbr.reshape(3, 8, 128, 16, 128).transpose(0, 3, 2, 1, 4)).reshape(48, 128, 8 * 128)
    sh["wo"] = _lhsT_tiles(g("w_o"), 16).reshape(16, 128, 16 * 128)
    wm = g("w_mem_kv")
    sh["wmk"] = _lhsT_tiles(wm[:, :1024], 16).reshape(8, 128, 16 * 128)
    sh["wmv"] = np.ascontiguousarray(wm[:, 1024:].reshape(16, 128, 2, 512).transpose(2, 1, 0, 3)).reshape(2, 128, 16 * 512)
    for t in ("k", "v"):
        w1 = g(f"cmp_w1_{t}").reshape(32, 64, 256).transpose(1, 0, 2)
        sh[f"cw1{t}"] = np.ascontiguousarray(np.concatenate([w1, w1], axis=0)).reshape(128, 32 * 256)
        sh[f"cw2{t}"] = np.ascontiguousarray(g(f"cmp_w2_{t}").reshape(2, 128, 64).transpose(1, 0, 2)).reshape(128, 128)
    sh.update(shared_constants())
    par = np.zeros((128, NPAR), f32)

    def put(name, arr):
        o, n = PCOL[name]
        par[:, o:o + n] = arr
    put("g_ffn1", g("ffn1_norm").reshape(16, 128).T)
    put("g_mix", g("mix_norm").reshape(16, 128).T)
    put("g_ffn2", g("ffn2_norm").reshape(16, 128).T)
    put("g_mem", g("mem_norm").reshape(16, 128).T)
    two = lambda v: np.concatenate([v, v])[:, None]
    put("g_q", two(g("nsa_q_norm")))
    put("g_ks", two(g("nsa_ks_norm")))
    put("g_kw", two(g("nsa_kw_norm")))
    put("g_kc", two(g("nsa_kc_norm")))
    put("g_mq", g("mem_q_norm").reshape(2, 128).T)
    put("g_mk", g("mem_k_norm").reshape(2, 128).T)
    put("convw", g("conv_w").reshape(3, 8, 128).transpose(2, 1, 0).reshape(128, 24))
    pek = g("cmp_pe_k").T
    pev = g("cmp_pe_v").T
    put("pek", np.concatenate([pek, pek], axis=0))
    put("pev", np.concatenate([pev, pev], axis=0))
    x = np.asarray(inp["x"], f32)
    mem = np.asarray(inp["mem"], f32)
    per = []
    for r in range(8):
        b, j = divmod(r, 4)
        d = {}
        xl = x[b].reshape(16, 4, 128, D)[:, j].reshape(2048, D)
        d["xT"] = np.ascontiguousarray(xl.T).reshape(16, 128, 2048)
        d["xTf"] = np.ascontiguousarray(x[b].T).reshape(16, 128, SEQ)
        d["memT"] = np.ascontiguousarray(mem[b].T).reshape(16, 128, 256)
        p = par.copy()
        o, n = PCOL["selc"]
        sel = np.zeros(4, f32)
        sel[j] = 1.0
        p[:, o:o + n] = sel[None, :]
        d["par"] = p
        d.update(build_constants(j))
        per.append(d)
    return sh, per


class Builder:
    def __init__(self, debug=None):
        self.debug = debug
        self.nc = bass.Bass("TRN2", target_bir_lowering=False)
        self.es = ExitStack()

    def dram_in(self, name, shape, dt=F32):
        return self.nc.dram_tensor(name, list(shape), dt, kind="ExternalInput").ap()

    def dram_out(self, name, shape, dt=F32):
        return self.nc.dram_tensor(name, list(shape), dt, kind="ExternalOutput").ap()

    def dram_tmp(self, name, shape, dt):
        return self.nc.dram_tensor(name, list(shape), dt).ap()

    def build(self, shared_shapes, percore_shapes):
        nc, es = self.nc, self.es
        with es:
            self.P = P = Prog(nc, es)
            self.I = {}
            for k, s in list(shared_shapes.items()) + list(percore_shapes.items()):
                self.I[k] = self.dram_in(k, s)
            self.out = self.dram_out("outT", [16, 128, 2048])
            self.outb = Buf("out")
            self.dbg = {}
            self.body()
            P.wait_all(SP, [self.outb] + [b for (_, b) in self.dbg.values()])
            P.emit()
        return nc

    def sb(self, st, name, shape, dt, n=1):
        self._uid = getattr(self, "_uid", 0) + 1
        return T(st.enter_context(self.nc.sbuf_tensor(f"s{self._uid}_" + name, list(shape), dt)), n, name)

    def body(self):
        nc, P, I = self.nc, self.P, self.I
        top = ExitStack()
        self.es.enter_context(top)
        self.W = {}
        self.Wb = {}

        def cast(name, piece=None):
            src = I[name]
            if name not in self.W:
                self.W[name] = self.dram_tmp(name + "_bf", src.shape, BF16)
                self.Wb[name] = [Buf(f"{name}{i}") for i in range(src.shape[0])] if len(src.shape) == 3 else [Buf(name)]
            dst = self.W[name]
            if len(src.shape) == 3:
                P.dma(POOL, dst[piece], src[piece], writes=[self.Wb[name][piece]])
            else:
                P.dma(POOL, dst, src, writes=[self.Wb[name][0]])
        self.cast = cast

        self.ps = [T(top.enter_context(nc.psum_tensor(f"ps{i}", [128, 512], F32)), 1, f"ps{i}") for i in range(8)]
        self.par = self.sb(top, "par", [128, NPAR], F32)
        P.dma(SP, self.par[:], I["par"], writes=self.par.b)
        self.gates = self.sb(top, "gates", [128, NT, 48], F32)
        self.ident = self.sb(top, "ident", [128, 128], BF16)
        self.onesf = self.sb(top, "onesf", [128, 128], F32)
        self.onesb = self.sb(top, "onesb", [128, 128], BF16)
        P.op(DVE, lambda e: e.memset(self.onesf[:], 1.0), writes=self.onesf.b)
        P.op(DVE, lambda e: e.memset(self.onesb[:], 1.0), writes=self.onesb.b)
        P.op(POOL, lambda e: e.memset(self.ident[:], 0.0), writes=self.ident.b)
        P.op(POOL, lambda e: e.affine_select(out=self.ident[:], in_=self.ident[:], compare_op=ALU.not_equal,
                                             fill=1.0, base=0, pattern=[[-1, 128]], channel_multiplier=1),
             reads=self.ident.b, writes=self.ident.b)
        self.x1_d = self.dram_tmp("x1_d", [16, 128, 2048], F32)
        self.x1_b = [Buf(f"x1d{s}") for s in range(NST)]
        self.qt_d = self.dram_tmp("qt_d", [NT, 128, 1024], BF16)
        self.qt_b = [Buf(f"qtd{i}") for i in range(NT)]
        self.yn_d = self.dram_tmp("yn_d", [NT, 128, 1024], BF16)
        self.yn_b = [Buf(f"ynd{i}") for i in range(NT)]
        self.gb_d = self.dram_tmp("gb_d", [128, 64 * XW], BF16)
        self.gb_b = Buf("gbd")
        self.epsc = self.sb(top, "epsc", [128, 1], F32)
        P.op(DVE, lambda e: e.memset(self.epsc[:], EPS), writes=self.epsc.b)

        for f in range(NFC):
            self.cast("wgu_ffn1", f)
            if f < 16:
                self.cast("wd_ffn1", f)
        for m in range(8, 24):
            self.cast("wf", m)
        for cb in (2, 3, 4, 0, 1):
            self.cast("wn", cb)
        self.cast("wng")
        later = []
        for nm in ("cw1k", "cw1v", "cw2k", "cw2v", "emat", "mapa"):
            later.append((nm, None))
        for m in list(range(0, 8)) + list(range(24, 80)):
            later.append(("wf", m))
        for m in range(48):
            later.append(("wbr", m))
        for m in range(16):
            later.append(("wo", m))
        for m in range(8):
            later.append(("wmk", m))
        for m in range(2):
            later.append(("wmv", m))
        for f in range(NFC):
            later.append(("wgu_ffn2", f))
            if f < 16:
                later.append(("wd_ffn2", f))
        self.later_casts = later
        self.phase1()
        while self.later_casts:
            self.cast(*self.later_casts.pop(0))
        if self.debug == "p1":
            self.dump_p1()
            return
        self.phase2()
        if self.debug == "p2":
            self.dump("d_yn", [NT, 128, 1024], BF16, self.yn_d, self.yn_b)
            self.dump("d_gb", [128, 64 * XW], BF16, self.gb_d, [self.gb_b])
            return
        self.phase3()

    def pcol(self, name, i=0, n=1):
        o, _ = PCOL[name]
        return self.par[:, o + i:o + i + n]

    class WStream:
        def __init__(self, bld, slots, reqs, depth=None):
            self.bld = bld
            self.slots = slots
            self.reqs = reqs
            self.issued = 0
            self.popped = 0
            self.depth = depth or len(slots)

        def _issue(self):
            while self.issued < len(self.reqs) and self.issued < self.popped + self.depth - 1:
                ap, deps = self.reqs[self.issued]
                slot = self.slots[self.issued % len(self.slots)]
                if len(ap.shape) == 3:
                    a, n = ap.shape[1], ap.shape[2]
                    dst = slot[:, 0:a * n].rearrange("p (a n) -> p a n", a=a)
                else:
                    dst = slot[:, 0:ap.shape[-1]]
                self.bld.P.dma(SP, dst, ap, reads=deps, writes=slot.b)
                self.issued += 1

        def pop(self):
            self._issue()
            slot = self.slots[self.popped % len(self.slots)]
            self.popped += 1
            return slot

    def rmsnorm_fm(self, xT, hT, gname, sq, rstd, psb, n=512):
        P = self.P
        for kc in range(16):
            s = sq[kc % 2]
            P.op(ACT, lambda e, s=s, kc=kc: e.activation(out=s[:, 0:n], in_=xT[:, kc, 0:n], func=AF.Square),
                 reads=xT.b, writes=s.b)
            P.op(PE, lambda e, s=s, kc=kc: e.matmul(psb[:, 0:n], lhsT=self.onesf[:], rhs=s[:, 0:n], start=(kc == 0), stop=(kc == 15)),
                 reads=s.b + self.onesf.b, writes=psb.b)
        P.op(ACT, lambda e: e.activation(out=rstd[:, 0:n], in_=psb[:, 0:n], func=AF.Sqrt, scale=1.0 / D, bias=self.epsc[:, 0:1]),
             reads=psb.b + self.epsc.b, writes=rstd.b)
        P.op(DVE, lambda e: e.reciprocal(out=rstd[:, 0:n], in_=rstd[:, 0:n]), reads=rstd.b, writes=rstd.b)
        o, _ = PCOL[gname]
        for kc in range(16):
            P.op(DVE, lambda e, kc=kc: e.scalar_tensor_tensor(out=hT[:, kc, 0:n], in0=xT[:, kc, 0:n], scalar=self.par[:, o + kc:o + kc + 1],
                                                             in1=rstd[:, 0:n], op0=ALU.mult, op1=ALU.mult),
                 reads=xT.b + rstd.b + self.par.b, writes=hT.b)

    def ffn(self, tag, xT, hT, actT, wsA, wsB, sg):
        P, ps = self.P, self.ps
        for f in range(NFC):
            w = wsA.pop()
            pg, pu = ps[f % 2], ps[2 + f % 2]
            wv = w[:, 0:4096].rearrange("p (a k c) -> p a k c", a=2, k=16)
            for kc in range(16):
                P.op(PE, lambda e, kc=kc, pg=pg, wv=wv: e.matmul(pg[:], lhsT=wv[:, 0, kc, :], rhs=hT[:, kc, :], start=(kc == 0), stop=(kc == 15)),
                     reads=w.b + hT.b, writes=pg.b)
            for kc in range(16):
                P.op(PE, lambda e, kc=kc, pu=pu, wv=wv: e.matmul(pu[:], lhsT=wv[:, 1, kc, :], rhs=hT[:, kc, :], start=(kc == 0), stop=(kc == 15)),
                     reads=w.b + hT.b, writes=pu.b)
            s = sg[f % 2]
            P.op(ACT, lambda e, s=s, pg=pg: e.activation(out=s[:], in_=pg[:], func=AF.Silu), reads=pg.b, writes=s.b)
            P.op(DVE, lambda e, s=s, pu=pu, f=f: e.tensor_tensor(out=actT[:, f, :], in0=pu[:], in1=s[:], op=ALU.mult),
                 reads=pu.b + s.b, writes=[actT.b[f]])
        for d in range(16):
            po = ps[4 + d % 2]
            for half in range(2):
                w = wsB.pop()
                wv = w[:, 0:22 * 128].rearrange("p (f c) -> p f c", f=22)
                for f2 in range(22):
                    f = 22 * half + f2
                    P.op(PE, lambda e, f=f, f2=f2, po=po, wv=wv: e.matmul(po[:], lhsT=wv[:, f2, :], rhs=actT[:, f, :], start=(f == 0), stop=(f == NFC - 1)),
                         reads=w.b + [actT.b[f]], writes=po.b)
            P.op(DVE, lambda e, d=d, po=po: e.scalar_tensor_tensor(out=xT[:, d, :], in0=po[:], scalar=0.5, in1=xT[:, d, :],
                                                                    op0=ALU.mult, op1=ALU.add),
                 reads=po.b + xT.b, writes=xT.b)

    def phase1(self):
        nc, P, I, ps = self.nc, self.P, self.I, self.ps
        nst = 16
        xsrc = I["xTf"]
        st = ExitStack()
        with st:
            xT = self.sb(st, "xT", [128, 16, 512], F32)
            hT = self.sb(st, "hT", [128, 16, 512], BF16)
            actT = self.sb(st, "actT", [128, NFC, 512], BF16, n=NFC)
            wA = [self.sb(st, f"wA{i}", [128, 4096], BF16) for i in range(4)]
            wB = [self.sb(st, f"wB{i}", [128, 22 * 128], BF16) for i in range(4)]
            sq = [self.sb(st, f"sq{i}", [128, 512], F32) for i in range(2)]
            sg = [self.sb(st, f"sg{i}", [128, 512], F32) for i in range(2)]
            rstd = self.sb(st, "rstd", [128, 512], F32)
            sqq = self.sb(st, "sqq", [128, 1024], F32)
            ss = self.sb(st, "ss", [128, 16], F32)
            qf = self.sb(st, "qf", [128, 1024], F32)
            qb = self.sb(st, "qb", [128, 1024], BF16)
            qts = [self.sb(st, f"qts{i}", [128, 1024], BF16) for i in range(2)]
            wng = self.sb(st, "wng", [128, 16 * 48], BF16)
            x1o = [self.sb(st, f"x1o{i}", [128, 16, 128], F32) for i in range(1)]
            h2o = self.sb(st, "h2o", [128, 16, 128], BF16)
            kf = self.sb(st, "kf", [128, 512], F32)
            kb = self.sb(st, "kb", [128, 512], BF16)
            xs = self.sb(st, "xs", [128, 4, XW], BF16, n=4)
            zu = self.sb(st, "zu", [128, 64], F32)
            P.op(DVE, lambda e: e.memset(xs[:], 1.0), writes=xs.b)

            reqA, reqB = [], []
            for s in range(nst):
                for f in range(NFC):
                    reqA.append((self.W["wgu_ffn1"][f], [self.Wb["wgu_ffn1"][f]]))
                for d in range(16):
                    reqB.append((self.W["wd_ffn1"][d][:, 0:2816], [self.Wb["wd_ffn1"][d]]))
                    reqB.append((self.W["wd_ffn1"][d][:, 2816:5632], [self.Wb["wd_ffn1"][d]]))
                for m in range(8, 24, 2):
                    reqA.append((self.W["wf"][m:m + 2].rearrange("m p c -> p m c"), [self.Wb["wf"][m], self.Wb["wf"][m + 1]]))
                for cb in (0, 1, 2, 3, 4):
                    reqA.append((self.W["wn"][cb][:, 0:4096], [self.Wb["wn"][cb]]))
                    reqA.append((self.W["wn"][cb][:, 4096:8192], [self.Wb["wn"][cb]]))
            wsA = self.WStream(self, wA, reqA)
            wsB = self.WStream(self, wB, reqB)
            P.dma(SP, wng[:], self.W["wng"], reads=self.Wb["wng"], writes=wng.b)
            wgv = wng[:].rearrange("p (k c) -> p k c", k=16)
            PB = [ps[0], ps[1], ps[4]]

            P.dma(ACT, xT[:], xsrc[:, :, 0:512].rearrange("k p t -> p k t"), writes=xT.b)
            for s in range(nst):
                self.rmsnorm_fm(xT, hT, "g_ffn1", sq, rstd, ps[6])
                self.ffn("ffn1", xT, hT, actT, wsA, wsB, sg)
                self.rmsnorm_fm(xT, hT, "g_mix", sq, rstd, ps[6])
                xo = x1o[0]
                for sl in range(4):
                    xin = xT[:, :, 128 * sl:128 * sl + 128]
                    hin = hT[:, :, 128 * sl:128 * sl + 128]
                    if sl == 0:
                        P.op(DVE, lambda e, xin=xin: e.tensor_scalar_mul(out=xo[:], in0=xin, scalar1=self.pcol("selc", 0)), reads=xT.b + self.par.b, writes=xo.b)
                        P.op(DVE, lambda e, hin=hin: e.tensor_scalar_mul(out=h2o[:], in0=hin, scalar1=self.pcol("selc", 0)), reads=hT.b + self.par.b, writes=h2o.b)
                    else:
                        P.op(DVE, lambda e, xin=xin, sl=sl: e.scalar_tensor_tensor(out=xo[:], in0=xin, scalar=self.pcol("selc", sl), in1=xo[:], op0=ALU.mult, op1=ALU.add),
                             reads=xT.b + self.par.b + xo.b, writes=xo.b)
                        P.op(DVE, lambda e, hin=hin, sl=sl: e.scalar_tensor_tensor(out=h2o[:], in0=hin, scalar=self.pcol("selc", sl), in1=h2o[:], op0=ALU.mult, op1=ALU.add),
                             reads=hT.b + self.par.b + h2o.b, writes=h2o.b)
                P.dma(POOL, self.x1_d[:, :, 128 * s:128 * s + 128].rearrange("k p t -> p k t"), xo[:], reads=xo.b, writes=[self.x1_b[s // 4]])
                if s + 1 < nst:
                    P.dma(ACT, xT[:], xsrc[:, :, 512 * (s + 1):512 * (s + 1) + 512].rearrange("k p t -> p k t"), writes=xT.b)
                hT_tail = hT[:].rearrange("p k (t c) -> p k t c", c=128)[:, :, :, 126:128]
                pz = ps[7]
                pzv = pz[:, 0:128].rearrange("p (m t c) -> p m t c", m=16, t=4)
                for mm in range(0, 16, 2):
                    w = wsA.pop()
                    wv = w[:, 0:4096].rearrange("p (m k c) -> p m k c", m=2, k=16)
                    for m2 in range(2):
                        for kc in range(16):
                            P.op(PE, lambda e, kc=kc, m2=m2, mm=mm, wv=wv: e.matmul(pzv[:, mm + m2], lhsT=wv[:, m2, kc, :], rhs=hT_tail[:, kc],
                                                                              start=(kc == 0), stop=(kc == 15)),
                                 reads=w.b + hT.b, writes=pz.b)
                P.op(ACT, lambda e: e.copy(out=zu[:], in_=pz[:, 64:128]), reads=pz.b, writes=zu.b)
                P.op(DVE, lambda e: e.tensor_tensor(out=xs[:, :, X_ZC:X_ZC + 16].rearrange("p t (m c) -> p m t c", m=8),
                                                    in0=pz[:, 0:64].rearrange("p (m t c) -> p m t c", m=8, t=4),
                                                    in1=zu[:].rearrange("p (m t c) -> p m t c", m=8, t=4), op=ALU.mult),
                     reads=pz.b + zu.b, writes=xs.b)
                units = []

                def mk_proj(lhs_fn, lhs_b, wh, wvh, pp):
                    def mm():
                        for kc in range(16):
                            P.op(PE, lambda e, kc=kc: e.matmul(pp[:], lhsT=lhs_fn(kc), rhs=wvh[kc // 8][:, kc % 8, :], start=(kc == 0), stop=(kc == 15)),
                                 reads=wh[kc // 8].b + lhs_b, writes=pp.b)
                    return mm

                def post_q(cb, pp):
                    def f():
                        P.op(ACT, lambda e: e.copy(out=qf[:, 512 * cb:512 * cb + 512], in_=pp[:]), reads=pp.b, writes=qf.b)
                        if cb == 1:
                            self.norm_heads(qf[:], qf.b, 16, sqq, ss, qb)
                            pt = ps[6]
                            ptv = pt[:].bitcast(BF16)
                            for m in range(8):
                                P.op(PE, lambda e, m=m: e.transpose(ptv[:, 128 * m:128 * m + 128], qb[:, 128 * m:128 * m + 128], self.ident[:]),
                                     reads=qb.b + self.ident.b, writes=pt.b)
                            qs = qts[s % 2]
                            P.op(ACT, lambda e: e.activation(out=qs[:], in_=ptv, func=AF.Copy, scale=self.pcol("g_q")), reads=pt.b + self.par.b, writes=qs.b)
                            P.dma(POOL, self.qt_d[s], qs[:], reads=qs.b, writes=[self.qt_b[s]])
                    return f

                def post_gate(pp):
                    def f():
                        P.op(ACT, lambda e, s=s: e.activation(out=self.gates[:, s, :], in_=pp[:, 0:48], func=AF.Sigmoid), reads=pp.b, writes=self.gates.b)
                    return f

                def post_kv(cb, tl, pp):
                    def f():
                        pt = ps[7]
                        ptv = pt[:].bitcast(BF16)
                        if cb == 2:
                            P.op(ACT, lambda e: e.copy(out=kf[:], in_=pp[:]), reads=pp.b, writes=kf.b)
                            self.norm_heads(kf[:], kf.b, 8, sqq, ss, kb)
                            for m in range(4):
                                P.op(PE, lambda e, m=m: e.transpose(ptv[:, 128 * m:128 * m + 128], kb[:, 128 * m:128 * m + 128], self.ident[:]),
                                     reads=kb.b + self.ident.b, writes=pt.b)
                            P.op(ACT, lambda e: e.activation(out=xs[:, tl, X_KS:X_KS + 256], in_=ptv[:, 0:256], func=AF.Copy, scale=self.pcol("g_ks")),
                                 reads=pt.b + self.par.b, writes=[xs.b[tl]])
                            P.op(ACT, lambda e: e.activation(out=xs[:, tl, X_KW:X_KW + 256], in_=ptv[:, 256:512], func=AF.Copy, scale=self.pcol("g_kw")),
                                 reads=pt.b + self.par.b, writes=[xs.b[tl]])
                        elif cb == 3:
                            P.op(ACT, lambda e: e.copy(out=xs[:, tl, 0:520].rearrange("p (g c) -> p g c", c=65)[:, :, 0:64],
                                                       in_=pp[:].rearrange("p (g c) -> p g c", c=64)),
                                 reads=pp.b, writes=[xs.b[tl]])
                        else:
                            P.op(ACT, lambda e: e.copy(out=kb[:], in_=pp[:]), reads=pp.b, writes=kb.b)
                            for m in range(4):
                                P.op(PE, lambda e, m=m: e.transpose(ptv[:, 128 * m:128 * m + 128], kb[:, 128 * m:128 * m + 128], self.ident[:]),
                                     reads=kb.b + self.ident.b, writes=pt.b)
                            P.op(DVE, lambda e: e.tensor_copy(out=xs[:, tl, X_KC:X_KC + 512], in_=ptv[:, 0:512]), reads=pt.b, writes=[xs.b[tl]])
                    return f

                ui = [0]

                def nextpp():
                    pp = PB[ui[0] % 3]
                    ui[0] += 1
                    return pp

                prev_post = None

                def run_unit(mm, post):
                    nonlocal prev_post
                    mm()
                    if prev_post is not None:
                        prev_post()
                    prev_post = post

                for cb in range(5):
                    wh = [wsA.pop(), wsA.pop()]
                    wvh = [w_[:, 0:4096].rearrange("p (k c) -> p k c", k=8) for w_ in wh]
                    if cb < 2:
                        pp = nextpp()
                        run_unit(mk_proj(lambda kc: h2o[:, kc, :], h2o.b, wh, wvh, pp), post_q(cb, pp))
                        if cb == 1:
                            pp = nextpp()

                            def mmg(pp=pp):
                                for kc in range(16):
                                    P.op(PE, lambda e, kc=kc: e.matmul(pp[:, 0:48], lhsT=h2o[:, kc, :], rhs=wgv[:, kc, :], start=(kc == 0), stop=(kc == 15)),
                                         reads=wng.b + h2o.b, writes=pp.b)
                            run_unit(mmg, post_gate(pp))
                    else:
                        for tl in range(4):
                            pp = nextpp()
                            run_unit(mk_proj(lambda kc, tl=tl: hT[:, kc, 128 * tl:128 * tl + 128], hT.b, wh, wvh, pp), post_kv(cb, tl, pp))
                prev_post()
                P.dma(POOL, self.gb_d[:, 4 * s * XW:(4 * s + 4) * XW], xs[:].rearrange("p t c -> p (t c)"), reads=xs.b, writes=[self.gb_b])
                ncast = (len(self.later_casts) + (nst - 1 - s)) // (nst - s) if s < nst - 1 else len(self.later_casts)
                for _ in range(ncast):
                    self.cast(*self.later_casts.pop(0))
            P.barrier()

    def phase2(self):
        nc, P, I, ps = self.nc, self.P, self.I, self.ps
        gbv = self.gb_d.rearrange("p (t c) -> p t c", c=XW)
        keep = ExitStack()
        self.es.enter_context(keep)
        KCT = self.sb(keep, "KCT", [128, 2, 512], BF16)
        VCA = self.sb(keep, "VCA", [128, 4, 260], BF16)
        P.op(DVE, lambda e: e.memset(KCT[:], 0.0), writes=KCT.b)
        P.op(DVE, lambda e: e.memset(VCA[:], 1.0), writes=VCA.b)
        st = ExitStack()
        with st:
            KC = self.sb(st, "KC", [128, 2, SEQ], BF16)
            VC = self.sb(st, "VC", [128, 2, SEQ], BF16)
            w1 = {"k": self.sb(st, "w1k", [128, 32, 256], BF16), "v": self.sb(st, "w1v", [128, 32, 256], BF16)}
            w2 = {"k": self.sb(st, "w2k", [128, 2, 64], BF16), "v": self.sb(st, "w2v", [128, 2, 64], BF16)}
            KCd = self.sb(st, "KCd", [128, 2, 16, 512], BF16)
            VCd = self.sb(st, "VCd", [128, 2, 16, 512], BF16)
            peb = self.sb(st, "peb", [128, 64], BF16)
            hid = [self.sb(st, f"hid{i}", [128, 512], BF16) for i in range(2)]
            hbias = self.sb(st, "hbias", [128, 2], F32)
            kcf = self.sb(st, "kcf", [128, 512], F32)
            ksq = self.sb(st, "ksq", [128, 512], F32)
            krs = self.sb(st, "krs", [128, 512], F32)
            for t in ("k", "v"):
                P.dma(SP, w1[t][:].rearrange("p l c -> p (l c)"), self.W[f"cw1{t}"], reads=self.Wb[f"cw1{t}"], writes=w1[t].b)
                P.dma(SP, w2[t][:].rearrange("p h c -> p (h c)"), self.W[f"cw2{t}"], reads=self.Wb[f"cw2{t}"], writes=w2[t].b)
            o, _ = PCOL["pek"]
            P.op(DVE, lambda e: e.tensor_copy(out=peb[:], in_=self.par[:, o:o + 64]), reads=self.par.b, writes=peb.b)
            for h_ in hid:
                P.op(DVE, lambda e, h_=h_: e.memset(h_[:], 0.0), writes=h_.b)
            for c in range(2):
                for q4 in range(4):
                    tsl = slice(16 * q4, 16 * q4 + 16)
                    P.dma(SP, KC[:, c, 2048 * q4:2048 * q4 + 2048].rearrange("p (t k) -> p t k", k=128),
                          gbv[:, tsl, X_KC + 128 * c:X_KC + 128 * c + 128], reads=[self.gb_b], writes=KC.b)
                    P.dma(SP, VC[:, c, 2048 * q4:2048 * q4 + 2048].rearrange("p (t k) -> p t k", k=128),
                          gbv[:, tsl, X_VC + 128 * c:X_VC + 128 * c + 128], reads=[self.gb_b], writes=VC.b)
            for c in range(2):
                P.op(DVE, lambda e, c=c: e.tensor_copy(out=KCd[:, c], in_=KC[:, c, :].rearrange("p (n r) -> p r n", r=16)), reads=KC.b, writes=KCd.b)
                P.op(DVE, lambda e, c=c: e.tensor_copy(out=VCd[:, c], in_=VC[:, c, :].rearrange("p (n r) -> p r n", r=16)), reads=VC.b, writes=VCd.b)
            for ti, t in enumerate(("k", "v")):
                src = KCd if t == "k" else VCd
                for g in range(4):
                    par_, c = g % 2, g // 2
                    pr = slice(64 * par_, 64 * par_ + 64)
                    pb_ = ps[6]
                    for hh in range(2):
                        for l in range(32):
                            P.op(PE, lambda e, hh=hh, l=l, pr=pr, t=t, ti=ti: e.matmul(pb_[:, hh:hh + 1], lhsT=w1[t][pr, l, 128 * hh:128 * hh + 128],
                                                                                 rhs=peb[pr, 32 * ti + l:32 * ti + l + 1], start=(l == 0), stop=(l == 31)),
                                 reads=w1[t].b + peb.b, writes=pb_.b)
                    P.op(ACT, lambda e: e.copy(out=hbias[:], in_=pb_[:, 0:2]), reads=pb_.b, writes=hbias.b)
                    for hh in range(2):
                        ph = ps[hh]
                        for l in range(32):
                            P.op(PE, lambda e, hh=hh, l=l, pr=pr, t=t, c=c, ph=ph, src=src: e.matmul(
                                ph[:, 0:511], lhsT=w1[t][pr, l, 128 * hh:128 * hh + 128], rhs=src[pr, c, l % 16, (l // 16):(l // 16) + 511],
                                start=(l == 0), stop=(l == 31)), reads=w1[t].b + src.b, writes=ph.b)
                        P.op(ACT, lambda e, hh=hh, ph=ph: e.activation(out=hid[hh][:, 0:511], in_=ph[:, 0:511], func=AF.Silu, bias=hbias[:, hh:hh + 1]),
                             reads=ph.b + hbias.b, writes=hid[hh].b)
                    if t == "k":
                        pk = ps[2]
                        for hh in range(2):
                            P.op(PE, lambda e, hh=hh, pr=pr: e.matmul(pk[pr, 0:511], lhsT=w2["k"][:, hh, :], rhs=hid[hh][:, 0:511], start=(hh == 0), stop=(hh == 1)),
                                 reads=w2["k"].b + hid[hh].b, writes=pk.b)
                        P.op(ACT, lambda e, pr=pr: e.copy(out=kcf[pr, 0:511], in_=pk[pr, 0:511]), reads=pk.b, writes=kcf.b)
                        P.op(DVE, lambda e, pr=pr: e.tensor_tensor(out=ksq[pr, 0:511], in0=kcf[pr, 0:511], in1=kcf[pr, 0:511], op=ALU.mult), reads=kcf.b, writes=ksq.b)
                        pn = ps[3]
                        P.op(PE, lambda e, pr=pr: e.matmul(pn[pr, 0:511], lhsT=self.onesf[pr, 0:64], rhs=ksq[pr, 0:511], start=True, stop=True),
                             reads=ksq.b + self.onesf.b, writes=pn.b)
                        P.op(ACT, lambda e, pr=pr: e.activation(out=krs[pr, 0:511], in_=pn[pr, 0:511], func=AF.Sqrt, scale=1.0 / 64, bias=self.epsc[pr, 0:1]),
                             reads=pn.b + self.epsc.b, writes=krs.b)
                        P.op(DVE, lambda e, pr=pr: e.reciprocal(out=krs[pr, 0:511], in_=krs[pr, 0:511]), reads=krs.b, writes=krs.b)
                        P.op(DVE, lambda e, pr=pr, c=c: e.scalar_tensor_tensor(out=KCT[pr, c, 0:511], in0=kcf[pr, 0:511], scalar=self.pcol("g_kc")[pr, :],
                                                                              in1=krs[pr, 0:511], op0=ALU.mult, op1=ALU.mult),
                             reads=kcf.b + krs.b + self.par.b, writes=KCT.b)
                    else:
                        for ncn in range(4):
                            pv = ps[4 + ncn % 2]
                            for hh in range(2):
                                P.op(PE, lambda e, hh=hh, ncn=ncn, pv=pv: e.matmul(pv[:, 0:64], lhsT=hid[hh][:, 128 * ncn:128 * ncn + 128], rhs=w2["v"][:, hh, :],
                                                                                 start=(hh == 0), stop=(hh == 1)),
                                     reads=w2["v"].b + hid[hh].b, writes=pv.b)
                            P.op(ACT, lambda e, ncn=ncn, pv=pv, g=g: e.copy(out=VCA[:, ncn, 65 * g:65 * g + 64], in_=pv[:, 0:64]), reads=pv.b, writes=VCA.b)
            P.barrier()
        st = ExitStack()
        with st:
            KS = self.sb(st, "KS", [128, 2, SEQ], BF16)
            KW = self.sb(st, "KW", [128, 2, SEQ], BF16)
            VS = self.sb(st, "VS", [128, 64, 260], BF16)
            VW = self.sb(st, "VW", [128, 64, 260], BF16)
            EM = self.sb(st, "EM", [128, 64, 128], BF16)
            MAPA = self.sb(st, "MAPA", [128, 4, 129], BF16)
            MW = self.sb(st, "MW", [128, 8, 128], BF16)
            MC = self.sb(st, "MC", [128, 32, 128], BF16)
            FB = self.sb(st, "FB", [128, 16, 128], F32)
            QT = [self.sb(st, f"QT{i}", [128, 8, 128], BF16) for i in range(2)]
            Pb = [self.sb(st, f"Pb{i}", [128, 512], BF16) for i in range(4)]
            yn = self.sb(st, "yn", [128, 1024], F32)
            ynb = self.sb(st, "ynb", [128, 1024], BF16)
            ynT = [self.sb(st, f"ynT{i}", [128, 1024], BF16) for i in range(2)]
            tmp = self.sb(st, "tmp", [128, 256], F32)
            rden = self.sb(st, "rden", [128, 8], F32)
            coef = self.sb(st, "coef", [128, 4], F32)
            imp = self.sb(st, "imp", [128, 128], F32)
            sc2 = self.sb(st, "sc2", [128, 128], F32)
            m8 = self.sb(st, "m8", [128, 16], F32)
            sbb = self.sb(st, "sbb", [128, 128], BF16)
            bT = self.sb(st, "bT", [128, 128], BF16)
            P.dma(SP, EM[:].rearrange("p t k -> p (t k)"), self.W["emat"], reads=self.Wb["emat"], writes=EM.b)
            P.dma(SP, MAPA[:].rearrange("p t k -> p (t k)"), self.W["mapa"], reads=self.Wb["mapa"], writes=MAPA.b)
            P.dma(SP, FB[:].rearrange("p t k -> p (t k)"), I["fbias"], writes=FB.b)
            P.dma(POOL, MW[:].rearrange("p t k -> p (t k)"), I["maskw"], writes=MW.b)
            P.dma(POOL, MC[:].rearrange("p t k -> p (t k)"), I["maskc"], writes=MC.b)
            for c in range(2):
                for q4 in range(4):
                    tsl = slice(16 * q4, 16 * q4 + 16)
                    P.dma(SP, KS[:, c, 2048 * q4:2048 * q4 + 2048].rearrange("p (t k) -> p t k", k=128),
                          gbv[:, tsl, X_KS + 128 * c:X_KS + 128 * c + 128], reads=[self.gb_b], writes=KS.b)
                    P.dma(SP, KW[:, c, 2048 * q4:2048 * q4 + 2048].rearrange("p (t k) -> p t k", k=128),
                          gbv[:, tsl, X_KW + 128 * c:X_KW + 128 * c + 128], reads=[self.gb_b], writes=KW.b)
            for q4 in range(4):
                tsl = slice(16 * q4, 16 * q4 + 16)
                P.dma(SP, VS[:, tsl, :], gbv[:, tsl, X_VS:X_VS + 260], reads=[self.gb_b], writes=VS.b)
                P.dma(SP, VW[:, tsl, :], gbv[:, tsl, X_VW:X_VW + 260], reads=[self.gb_b], writes=VW.b)
            pbi = [0]
            SB_ = [ps[0], ps[1], ps[6]]
            DEPTH = 2

            def stageA(t):
                psc = SB_[pbi[0] % 3]
                pb = Pb[pbi[0] % 4]
                pbi[0] += 1
                extra = t["extra"]
                nex = len(extra)
                kT, qrhs = t["kT"], t["qrhs"]
                P.op(PE, lambda e: e.matmul(psc[:].rearrange("p (a b) -> p a b", a=4), lhsT=kT, rhs=qrhs, start=True, stop=(nex == 0)),
                     reads=t["kT_b"] + t["qT_b"], writes=psc.b)
                for xi, (l_, r_, bufs) in enumerate(extra):
                    P.op(PE, lambda e, l_=l_, r_=r_, xi=xi: e.matmul(psc[:].rearrange("p (a b) -> p a b", a=4), lhsT=l_, rhs=r_, start=False, stop=(xi == nex - 1)),
                         reads=bufs, writes=psc.b)
                P.op(ACT, lambda e: e.activation(out=pb[:], in_=psc[:], func=AF.Exp, scale=0.125), reads=psc.b, writes=pb.b)
                return pb

            def stageB(po, t, pb, first, pimp):
                vrhs = t["vrhs"]
                for hl in range(4):
                    P.op(PE, lambda e, hl=hl: e.matmul(po[:, 65 * hl:65 * hl + 65], lhsT=pb[:, 128 * hl:128 * hl + 128], rhs=vrhs,
                                                       start=(first and hl == 0), stop=False, skip_group_check=True),
                         reads=pb.b + t["v_b"], writes=po.b)
                if t.get("imp_rhs") is not None:
                    imp_rhs = t["imp_rhs"]
                    for hl in range(4):
                        pi_ = pimp[hl // 2]
                        P.op(PE, lambda e, hl=hl, pi_=pi_: e.matmul(pi_[:, 129 * (hl % 2):129 * (hl % 2) + 129], lhsT=pb[:, 128 * hl:128 * hl + 128], rhs=imp_rhs,
                                                                   start=(first and hl % 2 == 0), stop=False, skip_group_check=True),
                             reads=pb.b + MAPA.b, writes=pi_.b)

            def run_branch(po, tiles, pimp=None):
                pbs = []
                n = len(tiles)
                for idx in range(n + DEPTH):
                    if idx < n:
                        pbs.append(stageA(tiles[idx]))
                    if idx >= DEPTH:
                        stageB(po, tiles[idx - DEPTH], pbs[idx - DEPTH], idx - DEPTH == 0, pimp)

            def epilogue(po, i, g, br):
                pov = po[:, 0:260].rearrange("p (h c) -> p h c", c=65)
                P.op(DVE, lambda e: e.tensor_scalar_max(out=rden[:, 0:4], in0=pov[:, :, 64], scalar1=1e-30), reads=po.b, writes=rden.b)
                P.op(DVE, lambda e: e.reciprocal(out=rden[:, 0:4], in_=rden[:, 0:4]), reads=rden.b, writes=rden.b)
                gv = self.gates[:, i, :].rearrange("p (h b) -> p h b", b=3)[:, 4 * g:4 * g + 4, br]
                P.op(DVE, lambda e: e.tensor_tensor(out=coef[:], in0=rden[:, 0:4], in1=gv, op=ALU.mult), reads=rden.b + self.gates.b, writes=coef.b)
                dst = yn[:, 256 * g:256 * g + 256].rearrange("p (h d) -> p h d", d=64)
                cb_ = coef[:].unsqueeze(2).broadcast_to([128, 4, 64])
                if br == 0:
                    P.op(DVE, lambda e: e.tensor_tensor(out=dst, in0=pov[:, :, 0:64], in1=cb_, op=ALU.mult), reads=po.b + coef.b, writes=yn.b)
                else:
                    tv = tmp[:].rearrange("p (h d) -> p h d", d=64)
                    P.op(DVE, lambda e: e.tensor_tensor(out=tv, in0=pov[:, :, 0:64], in1=cb_, op=ALU.mult), reads=po.b + coef.b, writes=tmp.b)
                    P.op(DVE, lambda e: e.tensor_tensor(out=dst, in0=dst, in1=tv, op=ALU.add), reads=tmp.b + yn.b, writes=yn.b)

            for i in range(NT):
                qt = QT[i % 2]
                P.dma(SP, qt[:].rearrange("p m k -> p (m k)"), self.qt_d[i], reads=[self.qt_b[i]], writes=qt.b)
                for g in range(4):
                    par_, c = g % 2, g // 2
                    pr = slice(64 * par_, 64 * par_ + 64)
                    qrhs = qt[pr, 4 * c:4 * c + 4, :]
                    base = dict(qrhs=qrhs, qT_b=qt.b)
                    pimp = [ps[4], ps[5]]
                    mch = cmp_mask_chunks(i)
                    tiles = []
                    for ch in cmp_chunks(i):
                        extra = []
                        if ch in mch:
                            slot = mch.index(ch)
                            extra.append((self.ident[:], MC[:, 2 * i + slot, :].unsqueeze(1).broadcast_to([128, 4, 128]), self.ident.b + MC.b))
                        tiles.append(dict(base, kT=KCT[pr, c, 128 * ch:128 * ch + 128], kT_b=KCT.b, extra=extra,
                                          vrhs=VCA[:, ch, 65 * g:65 * g + 65], v_b=VCA.b, imp_rhs=MAPA[:, ch, :]))
                    run_branch(ps[2], tiles, pimp)
                    epilogue(ps[2], i, g, 0)
                    for hl in range(4):
                        pi_ = pimp[hl // 2]
                        o_ = 129 * (hl % 2)
                        P.op(DVE, lambda e, pi_=pi_, o_=o_, hl=hl: e.tensor_scalar_max(out=rden[:, 4 + hl:5 + hl], in0=pi_[:, o_ + 128:o_ + 129], scalar1=1e-30),
                             reads=pi_.b, writes=rden.b)
                    P.op(DVE, lambda e: e.reciprocal(out=rden[:, 4:8], in_=rden[:, 4:8]), reads=rden.b, writes=rden.b)
                    for hl in range(4):
                        pi_ = pimp[hl // 2]
                        o_ = 129 * (hl % 2)
                        if hl == 0:
                            P.op(DVE, lambda e, pi_=pi_, o_=o_: e.scalar_tensor_tensor(out=imp[:], in0=pi_[:, o_:o_ + 128], scalar=rden[:, 4:5], in1=FB[:, i, :], op0=ALU.mult, op1=ALU.add),
                                 reads=pi_.b + rden.b + FB.b, writes=imp.b)
                        else:
                            P.op(DVE, lambda e, pi_=pi_, o_=o_, hl=hl: e.scalar_tensor_tensor(out=imp[:], in0=pi_[:, o_:o_ + 128], scalar=rden[:, 4 + hl:5 + hl], in1=imp[:], op0=ALU.mult, op1=ALU.add),
                                 reads=pi_.b + rden.b + imp.b, writes=imp.b)
                    P.op(DVE, lambda e: e.max(out=m8[:, 0:8], in_=imp[:]), reads=imp.b, writes=m8.b)
                    P.op(DVE, lambda e: e.match_replace(out=sc2[:], in_to_replace=m8[:, 0:8], in_values=imp[:], imm_value=-3.0e38), reads=imp.b + m8.b, writes=sc2.b)
                    P.op(DVE, lambda e: e.max(out=m8[:, 8:16], in_=sc2[:]), reads=sc2.b, writes=m8.b)
                    P.op(DVE, lambda e: e.tensor_scalar(out=sbb[:], in0=imp[:], scalar1=m8[:, 15:16], scalar2=NEGB, op0=ALU.is_lt, op1=ALU.mult),
                         reads=imp.b + m8.b, writes=sbb.b)
                    tiles = []
                    for kt in range(max(4 * i - 4, 0), 4 * i + 4):
                        extra = [(self.ident[:], MW[:, kt - 4 * i + 4, :].unsqueeze(1).broadcast_to([128, 4, 128]), self.ident.b + MW.b)]
                        tiles.append(dict(base, kT=KW[pr, c, 128 * kt:128 * kt + 128], kT_b=KW.b, extra=extra, vrhs=VW[:, kt, 65 * g:65 * g + 65], v_b=VW.b))
                    run_branch(ps[3], tiles)
                    epilogue(ps[3], i, g, 2)
                    pt = ps[7]
                    ptv = pt[:].bitcast(BF16)
                    P.op(PE, lambda e, ptv=ptv: e.transpose(ptv[:, 0:128], sbb[:], self.ident[:]), reads=sbb.b + self.ident.b, writes=pt.b)
                    P.op(ACT, lambda e, ptv=ptv: e.copy(out=bT[:], in_=ptv[:, 0:128]), reads=pt.b, writes=bT.b)
                    tiles = []
                    for kt in range(4 * i + 4):
                        extra = [(EM[:, kt, :], bT[:].unsqueeze(1).broadcast_to([128, 4, 128]), EM.b + bT.b)]
                        if kt >= 4 * i:
                            extra.append((self.ident[:], MW[:, kt - 4 * i + 4, :].unsqueeze(1).broadcast_to([128, 4, 128]), self.ident.b + MW.b))
                        tiles.append(dict(base, kT=KS[pr, c, 128 * kt:128 * kt + 128], kT_b=KS.b, extra=extra, vrhs=VS[:, kt, 65 * g:65 * g + 65], v_b=VS.b))
                    run_branch(ps[2], tiles)
                    epilogue(ps[2], i, g, 1)
                P.op(ACT, lambda e: e.copy(out=ynb[:], in_=yn[:]), reads=yn.b, writes=ynb.b)
                pt = ps[7]
                ptv = pt[:].bitcast(BF16)
                for m in range(8):
                    P.op(PE, lambda e, m=m, ptv=ptv: e.transpose(ptv[:, 128 * m:128 * m + 128], ynb[:, 128 * m:128 * m + 128], self.ident[:]),
                         reads=ynb.b + self.ident.b, writes=pt.b)
                yt = ynT[i % 2]
                P.op(DVE, lambda e, yt=yt, ptv=ptv: e.tensor_copy(out=yt[:], in_=ptv), reads=pt.b, writes=yt.b)
                P.dma(POOL, self.yn_d[i], yt[:], reads=yt.b, writes=[self.yn_b[i]])
            P.barrier()

    def phase3(self):
        nc, P, I, ps = self.nc, self.P, self.I, self.ps
        gbv = self.gb_d.rearrange("p (t c) -> p t c", c=XW)
        st = ExitStack()
        with st:
            xT = self.sb(st, "xT3", [128, 16, 512], F32)
            hT = self.sb(st, "hT3", [128, 16, 512], BF16)
            actT = self.sb(st, "actT3", [128, NFC, 512], BF16, n=NFC)
            wA = [self.sb(st, f"wA3{i}", [128, 4096], BF16) for i in range(5)]
            wB = [self.sb(st, f"wB3{i}", [128, 22 * 128], BF16) for i in range(4)]
            sq = [self.sb(st, f"sq3{i}", [128, 512], F32) for i in range(2)]
            sg = [self.sb(st, f"sg3{i}", [128, 512], F32) for i in range(2)]
            rstd = self.sb(st, "rstd3", [128, 512], F32)
            us = self.sb(st, "us", [128, 512], F32)
            zce = self.sb(st, "zce", [128, 4, 130], F32)
            acc = self.sb(st, "acc", [128, 512], F32)
            macc = self.sb(st, "macc", [128, 512], F32)
            qm = [self.sb(st, f"qm{i}", [128, 512], F32) for i in range(2)]
            qmT = [self.sb(st, f"qmT{i}", [128, 512], BF16) for i in range(2)]
            Pm = [self.sb(st, f"Pm{i}", [128, 512], BF16) for i in range(2)]
            kmT = self.sb(st, "kmT", [128, 8, 256], BF16)
            kmf = [self.sb(st, f"kmf{i}", [128, 256], F32) for i in range(2)]
            vm = self.sb(st, "vm", [128, 2, 1024], BF16)
            cand = self.sb(st, "cand", [128, 16, 4, 16], BF16)
            halo = self.sb(st, "halo", [128, 16, 16], F32)
            ycv = [(actT[:, cc, :], [actT.b[cc]]) for cc in range(8)]
            ymm = [(actT[:, 8 + cc, :], [actT.b[8 + cc]]) for cc in range(8)]
            ynn = [(actT[:, 16 + cc, :], [actT.b[16 + cc]]) for cc in range(8)]
            mrg = [(actT[:, 24 + dm, :], [actT.b[24 + dm]]) for dm in range(16)]

            reqA, reqB = [], []
            for m in range(8):
                reqA.append((self.W["wmk"][m], [self.Wb["wmk"][m]]))
            for m in range(2):
                reqA.append((self.W["wmv"][m][:, 0:4096], [self.Wb["wmv"][m]]))
                reqA.append((self.W["wmv"][m][:, 4096:8192], [self.Wb["wmv"][m]]))
            for s in range(NST):
                for cc in range(8):
                    for m in (cc, 8 + cc, 16 + cc):
                        reqA.append((self.W["wf"][m], [self.Wb["wf"][m]]))
                for m in range(24, 32):
                    reqA.append((self.W["wf"][m], [self.Wb["wf"][m]]))
                for dm in range(16):
                    for n in range(3):
                        reqA.append((self.W["wf"][32 + 16 * n + dm], [self.Wb["wf"][32 + 16 * n + dm]]))
                        reqA.append((self.W["wbr"][16 * n + dm], [self.Wb["wbr"][16 * n + dm]]))
                for dm in range(16):
                    reqA.append((self.W["wo"][dm], [self.Wb["wo"][dm]]))
                for f in range(NFC):
                    reqA.append((self.W["wgu_ffn2"][f], [self.Wb["wgu_ffn2"][f]]))
                for d in range(16):
                    reqB.append((self.W["wd_ffn2"][d][:, 0:2816], [self.Wb["wd_ffn2"][d]]))
                    reqB.append((self.W["wd_ffn2"][d][:, 2816:5632], [self.Wb["wd_ffn2"][d]]))
            wsA = self.WStream(self, wA, reqA)
            wsB = self.WStream(self, wB, reqB)

            P.dma(SP, xT[:, :, 0:256], I["memT"].rearrange("k p t -> p k t"), writes=xT.b)
            self.rmsnorm_fm(xT, hT, "g_mem", sq, rstd, ps[6], n=256)
            for h in range(4):
                for dc in range(2):
                    m = 2 * h + dc
                    w = wsA.pop()
                    wv = w[:, 0:2048].rearrange("p (k c) -> p k c", k=16)
                    pk = ps[dc]
                    for kc in range(16):
                        P.op(PE, lambda e, kc=kc, pk=pk, wv=wv: e.matmul(pk[:, 0:256], lhsT=wv[:, kc, :], rhs=hT[:, kc, 0:256], start=(kc == 0), stop=(kc == 15)),
                             reads=w.b + hT.b, writes=pk.b)
                    P.op(ACT, lambda e, dc=dc, pk=pk: e.copy(out=kmf[dc][:], in_=pk[:, 0:256]), reads=pk.b, writes=kmf[dc].b)
                    P.op(DVE, lambda e, dc=dc: e.tensor_tensor(out=sq[dc][:, 0:256], in0=kmf[dc][:], in1=kmf[dc][:], op=ALU.mult), reads=kmf[dc].b, writes=sq[dc].b)
                pn = ps[2]
                for dc in range(2):
                    P.op(PE, lambda e, dc=dc: e.matmul(pn[:, 0:256], lhsT=self.onesf[:], rhs=sq[dc][:, 0:256], start=(dc == 0), stop=(dc == 1)),
                         reads=sq[dc].b + self.onesf.b, writes=pn.b)
                P.op(ACT, lambda e: e.activation(out=rstd[:, 0:256], in_=pn[:, 0:256], func=AF.Sqrt, scale=1.0 / 256, bias=self.epsc[:, 0:1]),
                     reads=pn.b + self.epsc.b, writes=rstd.b)
                P.op(DVE, lambda e: e.reciprocal(out=rstd[:, 0:256], in_=rstd[:, 0:256]), reads=rstd.b, writes=rstd.b)
                for dc in range(2):
                    P.op(DVE, lambda e, dc=dc, h=h: e.scalar_tensor_tensor(out=kmT[:, 2 * h + dc, :], in0=kmf[dc][:], scalar=self.pcol("g_mk", dc), in1=rstd[:, 0:256],
                                                                          op0=ALU.mult, op1=ALU.mult),
                         reads=kmf[dc].b + rstd.b + self.par.b, writes=kmT.b)
            for vb in range(2):
                wh = [wsA.pop(), wsA.pop()]
                wvh = [w_[:, 0:4096].rearrange("p (k c) -> p k c", k=8) for w_ in wh]
                for mc in range(2):
                    pv = ps[mc]
                    for kc in range(16):
                        P.op(PE, lambda e, kc=kc, pv=pv, wvh=wvh, mc=mc: e.matmul(pv[:], lhsT=hT[:, kc, 128 * mc:128 * mc + 128], rhs=wvh[kc // 8][:, kc % 8, :], start=(kc == 0), stop=(kc == 15)),
                             reads=wh[kc // 8].b + hT.b, writes=pv.b)
                    P.op(ACT, lambda e, pv=pv, mc=mc, vb=vb: e.copy(out=vm[:, mc, 512 * vb:512 * vb + 512], in_=pv[:]), reads=pv.b, writes=vm.b)
            P.op(DVE, lambda e: e.memset(cand[:], 0.0), writes=cand.b)
            for jj in range(4):
                if jj == 0:
                    P.dma(SP, cand[:, 1:16, 0, :], gbv[:, 3:63:4, X_ZC:X_ZC + 16], reads=[self.gb_b], writes=cand.b)
                else:
                    P.dma(SP, cand[:, :, jj, :], gbv[:, jj - 1:64:4, X_ZC:X_ZC + 16], reads=[self.gb_b], writes=cand.b)
            for jj in range(4):
                if jj == 0:
                    P.op(DVE, lambda e: e.tensor_scalar_mul(out=halo[:], in0=cand[:, :, 0, :], scalar1=self.pcol("selc", 0)),
                         reads=cand.b + self.par.b, writes=halo.b)
                else:
                    P.op(DVE, lambda e, jj=jj: e.scalar_tensor_tensor(out=halo[:], in0=cand[:, :, jj, :], scalar=self.pcol("selc", jj), in1=halo[:], op0=ALU.mult, op1=ALU.add),
                         reads=cand.b + self.par.b + halo.b, writes=halo.b)

            for s in range(NST):
                tsl = slice(512 * s, 512 * s + 512)
                P.dma(ACT, xT[:], self.x1_d[:, :, tsl].rearrange("k p t -> p k t"), reads=[self.x1_b[s]], writes=xT.b)
                self.rmsnorm_fm(xT, hT, "g_mix", sq, rstd, ps[6])
                for cc in range(8):
                    pb3 = []
                    for bi in range(3):
                        w = wsA.pop()
                        wv = w[:, 0:2048].rearrange("p (k c) -> p k c", k=16)
                        pp = ps[bi]
                        for kc in range(16):
                            P.op(PE, lambda e, kc=kc, pp=pp, wv=wv: e.matmul(pp[:], lhsT=wv[:, kc, :], rhs=hT[:, kc, :], start=(kc == 0), stop=(kc == 15)),
                                 reads=w.b + hT.b, writes=pp.b)
                        pb3.append(pp)
                    pbg, pc, pu = pb3
                    P.op(ACT, lambda e, pu=pu: e.copy(out=us[:], in_=pu[:]), reads=pu.b, writes=us.b)
                    P.op(DVE, lambda e, pc=pc: e.tensor_tensor(out=zce[:, :, 2:130], in0=pc[:].rearrange("p (t k) -> p t k", k=128),
                                                              in1=us[:].rearrange("p (t k) -> p t k", k=128), op=ALU.mult),
                         reads=pc.b + us.b, writes=zce.b)
                    P.op(DVE, lambda e, cc=cc, s=s: e.tensor_copy(out=zce[:, :, 0:2], in_=halo[:, 4 * s:4 * s + 4, 2 * cc:2 * cc + 2]), reads=halo.b, writes=zce.b)
                    av = acc[:].rearrange("p (t k) -> p t k", k=128)
                    P.op(DVE, lambda e, cc=cc: e.tensor_scalar_mul(out=av, in0=zce[:, :, 2:130], scalar1=self.pcol("convw", 3 * cc + 2)),
                         reads=zce.b + self.par.b, writes=acc.b)
                    P.op(DVE, lambda e, cc=cc: e.scalar_tensor_tensor(out=av, in0=zce[:, :, 1:129], scalar=self.pcol("convw", 3 * cc + 1), in1=av, op0=ALU.mult, op1=ALU.add),
                         reads=zce.b + self.par.b + acc.b, writes=acc.b)
                    P.op(DVE, lambda e, cc=cc: e.scalar_tensor_tensor(out=av, in0=zce[:, :, 0:128], scalar=self.pcol("convw", 3 * cc + 0), in1=av, op0=ALU.mult, op1=ALU.add),
                         reads=zce.b + self.par.b + acc.b, writes=acc.b)
                    P.op(DVE, lambda e, cc=cc, pbg=pbg: e.tensor_tensor(out=ycv[cc][0], in0=pbg[:], in1=acc[:], op=ALU.mult), reads=pbg.b + acc.b, writes=ycv[cc][1])
                for h in range(4):
                    for dc in range(2):
                        w = wsA.pop()
                        wv = w[:, 0:2048].rearrange("p (k c) -> p k c", k=16)
                        pp = ps[dc]
                        for kc in range(16):
                            P.op(PE, lambda e, kc=kc, pp=pp, wv=wv: e.matmul(pp[:], lhsT=wv[:, kc, :], rhs=hT[:, kc, :], start=(kc == 0), stop=(kc == 15)),
                                 reads=w.b + hT.b, writes=pp.b)
                        P.op(ACT, lambda e, dc=dc, pp=pp: e.copy(out=qm[dc][:], in_=pp[:]), reads=pp.b, writes=qm[dc].b)
                        P.op(DVE, lambda e, dc=dc: e.tensor_tensor(out=sq[dc][:], in0=qm[dc][:], in1=qm[dc][:], op=ALU.mult), reads=qm[dc].b, writes=sq[dc].b)
                    pn = ps[2]
                    for dc in range(2):
                        P.op(PE, lambda e, dc=dc: e.matmul(pn[:], lhsT=self.onesf[:], rhs=sq[dc][:], start=(dc == 0), stop=(dc == 1)),
                             reads=sq[dc].b + self.onesf.b, writes=pn.b)
                    P.op(ACT, lambda e: e.activation(out=rstd[:], in_=pn[:], func=AF.Sqrt, scale=1.0 / 256, bias=self.epsc[:, 0:1]),
                         reads=pn.b + self.epsc.b, writes=rstd.b)
                    P.op(DVE, lambda e: e.reciprocal(out=rstd[:], in_=rstd[:]), reads=rstd.b, writes=rstd.b)
                    for dc in range(2):
                        P.op(DVE, lambda e, dc=dc: e.scalar_tensor_tensor(out=qmT[dc][:], in0=qm[dc][:], scalar=self.pcol("g_mq", dc), in1=rstd[:], op0=ALU.mult, op1=ALU.mult),
                             reads=qm[dc].b + rstd.b + self.par.b, writes=qmT[dc].b)
                    for mc in range(2):
                        pS = ps[3 + mc]
                        for dc in range(2):
                            P.op(PE, lambda e, dc=dc, mc=mc, pS=pS, h=h: e.matmul(pS[:], lhsT=kmT[:, 2 * h + dc, 128 * mc:128 * mc + 128], rhs=qmT[dc][:], start=(dc == 0), stop=(dc == 1)),
                                 reads=kmT.b + qmT[dc].b, writes=pS.b)
                        P.op(ACT, lambda e, mc=mc, pS=pS: e.activation(out=Pm[mc][:], in_=pS[:], func=AF.Exp, scale=1.0 / 16), reads=pS.b, writes=Pm[mc].b)
                    pd = ps[5]
                    for mc in range(2):
                        P.op(PE, lambda e, mc=mc: e.matmul(pd[:], lhsT=self.onesb[:], rhs=Pm[mc][:], start=(mc == 0), stop=(mc == 1)),
                             reads=Pm[mc].b + self.onesb.b, writes=pd.b)
                    P.op(DVE, lambda e: e.reciprocal(out=acc[:], in_=pd[:]), reads=pd.b, writes=acc.b)
                    for dc in range(2):
                        po = ps[dc]
                        for mc in range(2):
                            P.op(PE, lambda e, mc=mc, dc=dc, po=po, h=h: e.matmul(po[:], lhsT=vm[:, mc, 256 * h + 128 * dc:256 * h + 128 * dc + 128], rhs=Pm[mc][:], start=(mc == 0), stop=(mc == 1)),
                                 reads=vm.b + Pm[mc].b, writes=po.b)
                        P.op(DVE, lambda e, dc=dc, po=po, h=h: e.tensor_tensor(out=ymm[2 * h + dc][0], in0=po[:], in1=acc[:], op=ALU.mult),
                             reads=po.b + acc.b, writes=ymm[2 * h + dc][1])
                for cc in range(8):
                    P.dma(SP, ynn[cc][0].rearrange("p (t k) -> p t k", k=128), self.yn_d[4 * s:4 * s + 4, :, 128 * cc:128 * cc + 128].rearrange("t p k -> p t k"),
                          reads=self.yn_b[4 * s:4 * s + 4], writes=ynn[cc][1])
                ysrc = [ycv, ynn, ymm]
                for dm in range(16):
                    for n in range(3):
                        w = wsA.pop()
                        wv = w[:, 0:2048].rearrange("p (k c) -> p k c", k=16)
                        pa = ps[n % 2]
                        for kc in range(16):
                            P.op(PE, lambda e, kc=kc, pa=pa, wv=wv: e.matmul(pa[:], lhsT=wv[:, kc, :], rhs=hT[:, kc, :], start=(kc == 0), stop=(kc == 15)),
                                 reads=w.b + hT.b, writes=pa.b)
                        w2_ = wsA.pop()
                        wv2 = w2_[:, 0:1024].rearrange("p (k c) -> p k c", k=8)
                        pbr = ps[2 + n % 2]
                        for cc in range(8):
                            P.op(PE, lambda e, cc=cc, pbr=pbr, wv2=wv2, n=n: e.matmul(pbr[:], lhsT=wv2[:, cc, :], rhs=ysrc[n][cc][0], start=(cc == 0), stop=(cc == 7)),
                                 reads=w2_.b + ysrc[n][cc][1], writes=pbr.b)
                        sgt = sg[n % 2]
                        P.op(ACT, lambda e, pa=pa, sgt=sgt: e.activation(out=sgt[:], in_=pa[:], func=AF.Sigmoid), reads=pa.b, writes=sgt.b)
                        if n == 0:
                            P.op(DVE, lambda e, pbr=pbr, sgt=sgt: e.tensor_tensor(out=macc[:], in0=pbr[:], in1=sgt[:], op=ALU.mult), reads=pbr.b + sgt.b, writes=macc.b)
                        else:
                            P.op(DVE, lambda e, pbr=pbr, sgt=sgt: e.tensor_tensor(out=us[:], in0=pbr[:], in1=sgt[:], op=ALU.mult), reads=pbr.b + sgt.b, writes=us.b)
                            if n == 1:
                                P.op(DVE, lambda e: e.tensor_tensor(out=macc[:], in0=macc[:], in1=us[:], op=ALU.add), reads=us.b + macc.b, writes=macc.b)
                            else:
                                P.op(DVE, lambda e, dm=dm: e.tensor_tensor(out=mrg[dm][0], in0=macc[:], in1=us[:], op=ALU.add), reads=us.b + macc.b, writes=mrg[dm][1])
                for do in range(16):
                    w = wsA.pop()
                    wv = w[:, 0:2048].rearrange("p (k c) -> p k c", k=16)
                    po = ps[4 + do % 2]
                    for dm in range(16):
                        P.op(PE, lambda e, dm=dm, po=po, wv=wv: e.matmul(po[:], lhsT=wv[:, dm, :], rhs=mrg[dm][0], start=(dm == 0), stop=(dm == 15)),
                             reads=w.b + mrg[dm][1], writes=po.b)
                    P.op(DVE, lambda e, do=do, po=po: e.tensor_tensor(out=xT[:, do, :], in0=po[:], in1=xT[:, do, :], op=ALU.add), reads=po.b + xT.b, writes=xT.b)
                self.rmsnorm_fm(xT, hT, "g_ffn2", sq, rstd, ps[6])
                self.ffn("ffn2", xT, hT, actT, wsA, wsB, sg)
                P.dma(POOL, self.out[:, :, tsl].rearrange("k p t -> p k t"), xT[:], reads=xT.b, writes=[self.outb])


    def norm_heads(self, src, srcb, nh, sqq, ss, dst):
        P = self.P
        n = nh * 64
        P.op(DVE, lambda e: e.tensor_tensor(out=sqq[:, 0:n], in0=src, in1=src, op=ALU.mult), reads=srcb, writes=sqq.b)
        P.op(DVE, lambda e: e.tensor_reduce(out=ss[:, 0:nh], in_=sqq[:, 0:n].rearrange("p (h d) -> p h d", d=64), axis=AX.X, op=ALU.add),
             reads=sqq.b, writes=ss.b)
        P.op(ACT, lambda e: e.activation(out=ss[:, 0:nh], in_=ss[:, 0:nh], func=AF.Sqrt, scale=1.0 / 64, bias=self.epsc[:, 0:1]),
             reads=ss.b + self.epsc.b, writes=ss.b)
        P.op(DVE, lambda e: e.reciprocal(out=ss[:, 0:nh], in_=ss[:, 0:nh]), reads=ss.b, writes=ss.b)
        P.op(DVE, lambda e: e.tensor_tensor(out=dst[:, 0:n].rearrange("p (h d) -> p h d", d=64),
                                            in0=src.rearrange("p (h d) -> p h d", d=64),
                                            in1=ss[:, 0:nh].unsqueeze(2).broadcast_to([128, nh, 64]), op=ALU.mult),
             reads=srcb + ss.b, writes=dst.b)

    def dump(self, name, shape, dt, src_ap, bufs):
        o = self.dram_out(name, shape, dt)
        b = Buf(name)
        self.P.dma(SP, o, src_ap, reads=bufs, writes=[b])
        self.dbg[name] = (o, b)

    def dump_p1(self):
        self.dump("d_x1", [16, 128, 2048], F32, self.x1_d, self.x1_b)
        self.dump("d_qt", [NT, 128, 1024], BF16, self.qt_d, self.qt_b)
        self.dump("d_gb", [128, 64 * XW], BF16, self.gb_d, [self.gb_b])
        o = self.dram_out("d_gates", [128, NT * 48], F32)
        b = Buf("dg")
        self.P.dma(SP, o, self.gates[:].rearrange("p t c -> p (t c)"), reads=self.gates.b, writes=[b])
        self.dbg["d_gates"] = (o, b)


def run(inputs, debug=None, ncores=8):
    sh, per = prep_inputs(inputs)
    bld = Builder(debug)
    nc = bld.build({k: v.shape for k, v in sh.items()}, {k: v.shape for k, v in per[0].items()})
    in_maps = []
    for r in range(ncores):
        m = dict(sh)
        m.update(per[r])
        in_maps.append(m)
    import time
    t0 = time.time()
    import os
    res = run_bass_kernel_spmd(nc, in_maps, core_ids=list(range(ncores)), trace=bool(os.environ.get("KTRACE")))
    print("run_bass_kernel_spmd secs", time.time() - t0, "exec_ns", res.exec_time_ns, flush=True)
    return res


def kernel(**inputs):
    res = run(inputs)
    out = np.zeros((2, SEQ, D), np.float32)
    for r in range(8):
        b, j = divmod(r, 4)
        o = np.asarray(res.results[r]["outT"]).reshape(2048, 2048).T
        out[b].reshape(16, 4, 128, D)[:, j] = o.reshape(16, 128, D)
    return out
```
